# Optimizing a Trainium2 kernel written in Bass

```python
import math
import jax, jax.numpy as jnp
from jax import lax
import numpy as np

D_MODEL = 2048
BATCH = 2
SEQ = 8192
DEPTH = 2
DEC_BATCH = 4
DEC_SEQ = 8192
PAST_LEN = 128

HEAD_DIM = 128
N_META = 16
GRID_W = 64
NA_HEADS = 8
NA_KH_MAX = 8
NA_KW = 16
DIFF_HEADS = 8
DIFF_QK_DIM = HEAD_DIM // 2
GQA_Q_HEADS = 8
GQA_KV_HEADS = 2
GQA_REP = GQA_Q_HEADS // GQA_KV_HEADS
WINDOW = 128
BLOCK = 128
N_BRANCH = 3
NA_W = NA_HEADS * HEAD_DIM
DIFF_Q_W = 2 * DIFF_HEADS * DIFF_QK_DIM
DIFF_V_W = DIFF_HEADS * HEAD_DIM
GQA_Q_W = GQA_Q_HEADS * HEAD_DIM
GQA_KV_W = GQA_KV_HEADS * HEAD_DIM
BRANCH_W = 1024
IN_SIZES = (NA_W, NA_W, NA_W, DIFF_Q_W, DIFF_Q_W, DIFF_V_W, GQA_Q_W, GQA_KV_W, GQA_KV_W)
IN_W = 3 * NA_W + 2 * DIFF_Q_W + DIFF_V_W + GQA_Q_W + 2 * GQA_KV_W
FFN_HIDDEN = -(-(8 * D_MODEL) // 768) * 256
ROPE_THETA = 10000.0
RMS_EPS = 1e-6
NEG_INF = -1e30

kernel_name = 'hybrid_natten_diff_swa_encoder'


def rms_norm(x, g):
    xf = x.astype(jnp.float32)
    y = xf * lax.rsqrt(jnp.mean(xf * xf, axis=-1, keepdims=True) + RMS_EPS)
    return (y * g.astype(jnp.float32)).astype(x.dtype)


def rope(x, pos):
    half = x.shape[-1] // 2
    inv_freq = ROPE_THETA ** (-jnp.arange(half, dtype=jnp.float32) / half)
    ang = pos.astype(jnp.float32)[:, None] * inv_freq[None, :]
    cos = jnp.cos(ang)[None, :, None, :]
    sin = jnp.sin(ang)[None, :, None, :]
    xf = x.astype(jnp.float32)
    x1, x2 = xf[..., :half], xf[..., half:]
    return jnp.concatenate([x1 * cos - x2 * sin, x2 * cos + x1 * sin], axis=-1).astype(x.dtype)


def softmax32(s):
    return jax.nn.softmax(s.astype(jnp.float32), axis=-1)


def split_cols(p, sizes):
    out = []
    start = 0
    for s in sizes:
        out.append(p[..., start:start + s])
        start += s
    return out


def neighbourhood_attention(q, k, v, rpb):
    bsz, seq_len, heads, d = q.shape
    n = seq_len - N_META
    rows = n // GRID_W
    kh = min(NA_KH_MAX, rows)
    nk = kh * GRID_W
    scale = d ** -0.5
    dt = v.dtype
    qm, km, vm = q[:, :N_META], k[:, :N_META], v[:, :N_META]
    qr = q[:, N_META:].reshape(bsz, rows, GRID_W, heads, d)
    kr = k[:, N_META:].reshape(bsz, rows, GRID_W, heads, d)
    vr = v[:, N_META:].reshape(bsz, rows, GRID_W, heads, d)
    col = jnp.arange(GRID_W)
    col_start = jnp.clip(col - NA_KW // 2, 0, GRID_W - NA_KW)
    col_ok = (col[None, :] >= col_start[:, None]) & (col[None, :] < col_start[:, None] + NA_KW)
    col_ok = jnp.tile(col_ok, (1, kh))
    col_off = jnp.clip(col[None, :] - col[:, None], -(NA_KW - 1), NA_KW - 1) + (NA_KW - 1)

    def one_row(args):
        q_row, r = args
        start = jnp.clip(r - kh // 2, 0, rows - kh)
        k_win = lax.dynamic_slice_in_dim(kr, start, kh, axis=1).reshape(bsz, nk, heads, d)
        v_win = lax.dynamic_slice_in_dim(vr, start, kh, axis=1).reshape(bsz, nk, heads, d)
        row_off = start + jnp.arange(kh) - r + (NA_KH_MAX - 1)
        bias = rpb[:, row_off[None, :, None], col_off[:, None, :]].reshape(heads, GRID_W, nk)
        s_win = jnp.einsum('bqhd,bkhd->bhqk', q_row, k_win).astype(jnp.float32) * scale + bias.astype(jnp.float32)
        s_win = jnp.where(col_ok, s_win, NEG_INF)
        s_meta = jnp.einsum('bqhd,bmhd->bhqm', q_row, km).astype(jnp.float32) * scale
        p = softmax32(jnp.concatenate([s_win, s_meta], axis=-1)).astype(dt)
        return (jnp.einsum('bhqk,bkhd->bqhd', p[..., :nk], v_win)
                + jnp.einsum('bhqm,bmhd->bqhd', p[..., nk:], vm))

    o_r = lax.map(one_row, (jnp.moveaxis(qr, 1, 0), jnp.arange(rows)))
    o_r = jnp.moveaxis(o_r, 0, 1).reshape(bsz, n, heads, d)
    p_m = softmax32(jnp.einsum('bqhd,bkhd->bhqk', qm, km).astype(jnp.float32) * scale).astype(dt)
    o_m = jnp.einsum('bhqk,bkhd->bqhd', p_m, vm)
    return jnp.concatenate([o_m, o_r], axis=1).reshape(bsz, seq_len, heads * d)


def diff_attention(q, k, v, lam_vec, subln_g, lambda_init):
    bsz, seq_len = q.shape[:2]
    n = seq_len - N_META
    dt = v.dtype
    q = q.reshape(bsz, seq_len, DIFF_HEADS, 2, DIFF_QK_DIM)
    k = k.reshape(bsz, seq_len, DIFF_HEADS, 2, DIFF_QK_DIM)
    lv = lam_vec.astype(jnp.float32)
    lam = jnp.exp(jnp.sum(lv[0] * lv[1])) - jnp.exp(jnp.sum(lv[2] * lv[3])) + lambda_init
    scale = DIFF_QK_DIM ** -0.5

    def attend(q_blk):
        s = jnp.einsum('bqhcd,bkhcd->bhcqk', q_blk, k).astype(jnp.float32) * scale
        p = softmax32(s)
        p_diff = (p[:, :, 0] - lam * p[:, :, 1]).astype(dt)
        o = jnp.einsum('bhqk,bkhd->bqhd', p_diff, v)
        return rms_norm(o, subln_g) * (1.0 - lambda_init)

    o_m = attend(q[:, :N_META])
    q_blocks = jnp.moveaxis(q[:, N_META:].reshape(bsz, n // BLOCK, BLOCK, DIFF_HEADS, 2, DIFF_QK_DIM), 1, 0)
    o_r = lax.map(attend, q_blocks)
    o_r = jnp.moveaxis(o_r, 0, 1).reshape(bsz, n, DIFF_HEADS, HEAD_DIM)
    return jnp.concatenate([o_m, o_r], axis=1).reshape(bsz, seq_len, DIFF_HEADS * HEAD_DIM)


def window_gqa(q, k, v, sink):
    bsz, seq_len, _, d = q.shape
    n = seq_len - N_META
    nb = n // BLOCK
    nk = 3 * BLOCK
    scale = d ** -0.5
    dt = v.dtype
    q = q.reshape(bsz, seq_len, GQA_KV_HEADS, GQA_REP, d)
    sink_g = sink.astype(jnp.float32).reshape(GQA_KV_HEADS, GQA_REP)
    qm, km, vm = q[:, :N_META], k[:, :N_META], v[:, :N_META]
    pad = ((0, 0), (BLOCK, BLOCK), (0, 0), (0, 0))
    kp = jnp.pad(k[:, N_META:], pad)
    vp = jnp.pad(v[:, N_META:], pad)

    def one_block(args):
        q_blk, b = args
        k_win = lax.dynamic_slice_in_dim(kp, b * BLOCK, nk, axis=1)
        v_win = lax.dynamic_slice_in_dim(vp, b * BLOCK, nk, axis=1)
        q_pos = b * BLOCK + jnp.arange(BLOCK)
        k_pos = (b - 1) * BLOCK + jnp.arange(nk)
        band = (jnp.abs(q_pos[:, None] - k_pos[None, :]) <= WINDOW) & (k_pos[None, :] >= 0) & (k_pos[None, :] < n)
        s_band = jnp.where(band, jnp.einsum('bqgrd,bkgd->bgrqk', q_blk, k_win).astype(jnp.float32) * scale, NEG_INF)
        s_meta = jnp.einsum('bqgrd,bmgd->bgrqm', q_blk, km).astype(jnp.float32) * scale
        s_sink = jnp.broadcast_to(sink_g[None, :, :, None, None], s_meta.shape[:-1] + (1,))
        p = softmax32(jnp.concatenate([s_band, s_meta, s_sink], axis=-1)).astype(dt)
        return (jnp.einsum('bgrqk,bkgd->bqgrd', p[..., :nk], v_win)
                + jnp.einsum('bgrqm,bmgd->bqgrd', p[..., nk:nk + N_META], vm))

    q_blocks = jnp.moveaxis(q[:, N_META:].reshape(bsz, nb, BLOCK, GQA_KV_HEADS, GQA_REP, d), 1, 0)
    o_r = lax.map(one_block, (q_blocks, jnp.arange(nb)))
    o_r = jnp.moveaxis(o_r, 0, 1).reshape(bsz, n, GQA_KV_HEADS, GQA_REP, d)
    k1, v1 = k[:, N_META:N_META + BLOCK], v[:, N_META:N_META + BLOCK]
    near = ((N_META + jnp.arange(BLOCK))[None, :] - jnp.arange(N_META)[:, None]) <= WINDOW
    s_mm = jnp.einsum('bqgrd,bmgd->bgrqm', qm, km).astype(jnp.float32) * scale
    s_m1 = jnp.where(near, jnp.einsum('bqgrd,bkgd->bgrqk', qm, k1).astype(jnp.float32) * scale, NEG_INF)
    s_sink = jnp.broadcast_to(sink_g[None, :, :, None, None], s_mm.shape[:-1] + (1,))
    p = softmax32(jnp.concatenate([s_mm, s_m1, s_sink], axis=-1)).astype(dt)
    o_m = (jnp.einsum('bgrqm,bmgd->bqgrd', p[..., :N_META], vm)
           + jnp.einsum('bgrqk,bkgd->bqgrd', p[..., N_META:N_META + BLOCK], v1))
    return jnp.concatenate([o_m, o_r], axis=1).reshape(bsz, seq_len, GQA_Q_HEADS * d)


def encoder_trunk(x, meta_tokens, norm_mix, w_in, na_rel_bias, diff_lambda, diff_subln, gqa_sink,
                  w_gate, w_branch, w_out, norm_ffn, w_ffn_in, w_ffn_out, norm_final):
    bsz, n, dm = x.shape
    seq_len = N_META + n
    h = jnp.concatenate([jnp.broadcast_to(meta_tokens[None].astype(x.dtype), (bsz, N_META, dm)), x], axis=1)
    pos = jnp.arange(seq_len)
    for layer in range(DEPTH):
        u = rms_norm(h, norm_mix[layer])
        proj = u @ w_in[layer]
        qa, ka, va, qb, kb, vb, qc, kc, vc = split_cols(proj, IN_SIZES)
        o_a = neighbourhood_attention(qa.reshape(bsz, seq_len, NA_HEADS, HEAD_DIM),
                                      ka.reshape(bsz, seq_len, NA_HEADS, HEAD_DIM),
                                      va.reshape(bsz, seq_len, NA_HEADS, HEAD_DIM),
                                      na_rel_bias[layer])
        o_b = diff_attention(rope(qb.reshape(bsz, seq_len, 2 * DIFF_HEADS, DIFF_QK_DIM), pos),
                             rope(kb.reshape(bsz, seq_len, 2 * DIFF_HEADS, DIFF_QK_DIM), pos),
                             vb.reshape(bsz, seq_len, DIFF_HEADS, HEAD_DIM),
                             diff_lambda[layer], diff_subln[layer],
                             0.8 - 0.6 * math.exp(-0.3 * layer))
        o_c = window_gqa(rope(qc.reshape(bsz, seq_len, GQA_Q_HEADS, HEAD_DIM), pos),
                         rope(kc.reshape(bsz, seq_len, GQA_KV_HEADS, HEAD_DIM), pos),
                         vc.reshape(bsz, seq_len, GQA_KV_HEADS, HEAD_DIM),
                         gqa_sink[layer])
        branches = (o_a, o_b, o_c)
        merged = jax.nn.sigmoid(u @ w_gate[layer, 0]) * (branches[0] @ w_branch[layer, 0])
        for i in range(1, N_BRANCH):
            merged = merged + jax.nn.sigmoid(u @ w_gate[layer, i]) * (branches[i] @ w_branch[layer, i])
        h = h + merged @ w_out[layer]
        f = rms_norm(h, norm_ffn[layer])
        g_up = f @ w_ffn_in[layer]
        h = h + (jax.nn.silu(g_up[..., :FFN_HIDDEN]) * g_up[..., FFN_HIDDEN:]) @ w_ffn_out[layer]
    return rms_norm(h, norm_final)[:, N_META:]


def setup_inputs(seed: int = 0) -> dict:
    key = jax.random.key(seed)
    ks = jax.random.split(key, 16)
    f32 = jnp.float32

    def normal(k, shape, scale):
        return jax.random.normal(k, shape, f32) * scale

    return {
        'x_prompt': normal(ks[0], (BATCH, SEQ, D_MODEL), 1.0),
        'x_sample': normal(ks[1], (DEC_BATCH, DEC_SEQ, D_MODEL), 1.0),
        'meta_tokens': normal(ks[2], (N_META, D_MODEL), 1.0),
        'norm_mix': 1.0 + normal(ks[3], (DEPTH, D_MODEL), 0.02),
        'w_in': normal(ks[4], (DEPTH, D_MODEL, IN_W), D_MODEL ** -0.5),
        'na_rel_bias': normal(ks[5], (DEPTH, NA_HEADS, 2 * NA_KH_MAX - 1, 2 * NA_KW - 1), 0.1),
        'diff_lambda': normal(ks[6], (DEPTH, 4, DIFF_QK_DIM), 0.1),
        'diff_subln': 1.0 + normal(ks[7], (DEPTH, HEAD_DIM), 0.02),
        'gqa_sink': normal(ks[8], (DEPTH, GQA_Q_HEADS), 0.5),
        'w_gate': normal(ks[9], (DEPTH, N_BRANCH, D_MODEL, D_MODEL), D_MODEL ** -0.5),
        'w_branch': normal(ks[10], (DEPTH, N_BRANCH, BRANCH_W, D_MODEL), BRANCH_W ** -0.5),
        'w_out': normal(ks[11], (DEPTH, D_MODEL, D_MODEL), D_MODEL ** -0.5),
        'norm_ffn': 1.0 + normal(ks[12], (DEPTH, D_MODEL), 0.02),
        'w_ffn_in': normal(ks[13], (DEPTH, D_MODEL, 2 * FFN_HIDDEN), D_MODEL ** -0.5),
        'w_ffn_out': normal(ks[14], (DEPTH, FFN_HIDDEN, D_MODEL), FFN_HIDDEN ** -0.5),
        'norm_final': 1.0 + normal(ks[15], (D_MODEL,), 0.02),
    }


def reference(x_prompt, x_sample, meta_tokens, norm_mix, w_in, na_rel_bias, diff_lambda, diff_subln,
              gqa_sink, w_gate, w_branch, w_out, norm_ffn, w_ffn_in, w_ffn_out, norm_final):
    y_prompt = encoder_trunk(x_prompt, meta_tokens, norm_mix, w_in, na_rel_bias, diff_lambda, diff_subln,
                             gqa_sink, w_gate, w_branch, w_out, norm_ffn, w_ffn_in, w_ffn_out, norm_final)
    y_sample = encoder_trunk(x_sample, meta_tokens, norm_mix, w_in, na_rel_bias, diff_lambda, diff_subln,
                             gqa_sink, w_gate, w_branch, w_out, norm_ffn, w_ffn_in, w_ffn_out, norm_final)
    return (y_prompt, y_sample)
```

```python
import numpy as np
import concourse.bass as bass
import concourse.mybir as mybir

F32 = mybir.dt.float32
BF16 = mybir.dt.bfloat16
AF = mybir.ActivationFunctionType
ALU = mybir.AluOpType

COMPUTE = ("pe", "act", "dve", "pool")
NSLOT = 8


class Buf:
    __slots__ = ("name", "last_w", "readers", "dma_readers")

    def __init__(self, name):
        self.name = name
        self.last_w = None
        self.readers = {}
        self.dma_readers = []


class Op:
    __slots__ = ("eng", "fn", "is_dma", "idx", "waits", "signal", "sigval", "slot", "k")

    def __init__(self, eng, fn, is_dma):
        self.eng = eng
        self.fn = fn
        self.is_dma = is_dma
        self.waits = []
        self.signal = False
        self.sigval = None
        self.slot = None
        self.k = None


class _Rec:
    def __init__(self):
        self.call = None

    def __getattr__(self, name):
        def f(*a, **k):
            self.call = (name, a, k)
            return None
        return f


class Prog:
    def __init__(self):
        self.ops = {e: [] for e in ("pe", "act", "dve", "pool", "sp")}
        self.nops = 0
        self.waited = {e: {c: -1 for c in COMPUTE} for e in self.ops}
        self.waited_dma = {e: {} for e in self.ops}
        self.dma_count = {e: 0 for e in self.ops}
        self.dma_ops = {e: [] for e in self.ops}
        self.pending_barrier = {e: None for e in self.ops}
        self.cuts = []

    def add(self, eng, fn, reads=(), writes=(), dma=False):
        rec = _Rec()
        fn(rec)
        fn = (lambda e, c=rec.call: getattr(e, c[0])(*c[1], **c[2]))
        op = Op(eng, fn, dma)
        lst = self.ops[eng]
        op.k = len(lst)
        op.idx = self.nops
        self.nops += 1
        deps = []
        for b in reads:
            if b.last_w is not None:
                deps.append(b.last_w)
        for b in writes:
            if b.last_w is not None:
                deps.append(b.last_w)
            deps.extend(b.readers.values())
            deps.extend(b.dma_readers)
        pb = self.pending_barrier[eng]
        if pb is not None:
            deps.extend(pb)
            self.pending_barrier[eng] = None
        if dma:
            n = self.dma_count[eng]
            op.slot = n % NSLOT
            op.sigval = 16 * (n // NSLOT + 1)
            op.signal = True
            if n >= NSLOT:
                deps.append(self.dma_ops[eng][n - NSLOT])
            self.dma_count[eng] = n + 1
            self.dma_ops[eng].append(op)
        for b in writes:
            b.last_w = op
            b.readers = {}
            b.dma_readers = []
        for b in reads:
            if b.last_w is op:
                continue
            if dma:
                b.dma_readers.append(op)
            else:
                b.readers[eng] = op
        seen = set()
        for d in deps:
            if d is op or id(d) in seen:
                continue
            seen.add(id(d))
            if d.is_dma:
                key = (d.eng, d.slot)
                nk = d.sigval
                if self.waited_dma[eng].get(key, 0) >= nk:
                    continue
                self.waited_dma[eng][key] = nk
                op.waits.append(d)
            else:
                if d.eng == eng and not dma and eng == "pe":
                    continue
                if self.waited[eng][d.eng] >= d.k:
                    continue
                self.waited[eng][d.eng] = d.k
                d.signal = True
                op.waits.append(d)
        lst.append(op)
        return op

    def barrier(self):
        deps = []
        for e in COMPUTE:
            if self.ops[e]:
                for o in reversed(self.ops[e]):
                    if not o.is_dma:
                        deps.append(o)
                        break
        for e in self.ops:
            deps.extend(self.dma_ops[e][-NSLOT:])
        for e in self.ops:
            self.pending_barrier[e] = list(deps)
        self.cuts.append({e: len(self.ops[e]) for e in self.ops})

    def emit(self, nc, final_wait_ops=()):
        import contextlib
        with contextlib.ExitStack() as st:
            sems = {}
            for e in COMPUTE:
                sems[e] = st.enter_context(nc.semaphore("s_" + e))
            dsems = {}
            for e in self.ops:
                if self.dma_count[e]:
                    for s in range(NSLOT):
                        dsems[(e, s)] = st.enter_context(nc.semaphore("d_%s_%d" % (e, s)))
            for e in COMPUTE:
                c = 0
                for o in self.ops[e]:
                    if o.is_dma:
                        continue
                    if o.signal:
                        c += 1
                        o.sigval = c
            cuts = list(self.cuts) + [{e: len(self.ops[e]) for e in self.ops}]
            prev = {e: 0 for e in self.ops}
            for cut in cuts:
                def run(engname, lo, hi):
                    def body(eng):
                        for o in self.ops[engname][lo:hi]:
                            for d in o.waits:
                                if d.is_dma:
                                    eng.wait_ge(dsems[(d.eng, d.slot)], d.sigval)
                                else:
                                    eng.wait_ge(sems[d.eng], d.sigval)
                            ins = o.fn(eng)
                            if o.is_dma:
                                ins.then_inc(dsems[(o.eng, o.slot)], 16)
                            elif o.signal:
                                ins.then_inc(sems[o.eng], 1)
                    return body
                if all(cut[e] == prev[e] for e in self.ops):
                    continue
                with nc.Block() as block:
                    block.tensor(run("pe", prev["pe"], cut["pe"]))
                    block.scalar(run("act", prev["act"], cut["act"]))
                    block.vector(run("dve", prev["dve"], cut["dve"]))
                    block.gpsimd(run("pool", prev["pool"], cut["pool"]))
                    block.sync(run("sp", prev["sp"], cut["sp"]))
                prev = cut


class Arena:
    def __init__(self, raw_ap_u8, nbytes):
        self.raw = raw_ap_u8
        self.nbytes = nbytes
        self.off = 0
        self.cnt = 0

    def reset(self, off=0):
        self.off = off

    def alloc(self, shape_free, dtype, parts=128, name=None):
        esz = 4 if dtype == F32 else 2
        n = int(np.prod(shape_free))
        nb = (n * esz + 63) // 64 * 64
        assert self.off + nb <= self.nbytes, ("SBUF arena overflow", self.off, nb, self.nbytes)
        v = self.raw[0:parts, self.off:self.off + n * esz].bitcast(dtype)
        self.off += nb
        if len(shape_free) == 2:
            v = v.rearrange("p (a b) -> p a b", a=shape_free[0])
        elif len(shape_free) == 3:
            v = v.rearrange("p (a b c) -> p a b c", a=shape_free[0], b=shape_free[1])
        self.cnt += 1
        return v, Buf(name or ("t%d" % self.cnt))


from concourse.bass_utils import run_bass_kernel_spmd

U8 = mybir.dt.uint8
D = 2048
KC = 16
HD = 128
NMETA = 16
FFN = 5632
IN_W = 7680
EPS = 1e-6
NA_COMBOS = [(-4, "M-"), (-2, "F"), (0, "F"), (2, "F"), (4, "M+"), (-6, "F"), (-4, "F"), (4, "F"), (6, "F")]
ARENA_BYTES = 204 * 1024


def na_plan(rows):
    def start(r):
        return min(max(r - 4, 0), rows - 8)
    plan = []
    for rp in range(rows // 2):
        lst = []
        for kp in range(rows // 2):
            val = [[start(2 * rp + q) <= 2 * kp + k < start(2 * rp + q) + 8 for q in range(2)] for k in range(2)]
            flat = (val[0][0], val[0][1], val[1][0], val[1][1])
            if not any(flat):
                continue
            delta = 2 * (kp - rp)
            if all(flat):
                typ = "F"
            elif flat == (True, False, True, True):
                typ = "M-"
            elif flat == (False, True, False, False):
                typ = "M+"
            else:
                raise AssertionError(("na pattern", rp, kp, flat))
            lst.append((kp, NA_COMBOS.index((delta, typ))))
        plan.append(lst)
    return plan


def host_constants(N):
    L = N + NMETA
    c = {}
    c["ident"] = np.eye(128, dtype=np.float32)
    def perm(half, blk):
        m = np.zeros((128, 128), np.float32)
        for p in range(128):
            o = p % blk
            if o < half:
                m[p + half, p] = -1.0
            else:
                m[p - half, p] = 1.0
        return m
    c["pmB"] = perm(32, 64)
    c["pmC"] = perm(64, 128)
    pos = np.concatenate([np.arange(N) + NMETA, np.arange(NMETA)]).astype(np.float32)
    def tables(half):
        inv = (np.float32(10000.0) ** (-(np.arange(half, dtype=np.float32)) / np.float32(half))).astype(np.float32)
        fi = np.arange(128) % half
        ang = (pos[None, :] * inv[fi][:, None]).astype(np.float32)
        return np.cos(ang).astype(np.float32), np.sin(ang).astype(np.float32)
    c["cosB"], c["sinB"] = tables(32)
    c["cosC"], c["sinC"] = tables(64)
    k = np.arange(128)[:, None]
    q = np.arange(128)[None, :]
    mprev = (k >= q).astype(np.float32)
    mnext = (k <= q).astype(np.float32)
    c["mprev4"] = np.ascontiguousarray(np.broadcast_to(mprev[:, None, :], (128, 4, 128))).astype(np.float32)
    c["mnext4"] = np.ascontiguousarray(np.broadcast_to(mnext[:, None, :], (128, 4, 128))).astype(np.float32)
    qm = np.arange(16)[None, :]
    near = (k <= 112 + qm).astype(np.float32)
    c["near4"] = np.ascontiguousarray(np.broadcast_to(near[:, None, :], (128, 4, 16))).astype(np.float32)
    kc_ = np.arange(64)
    cs = np.clip(kc_ - 8, 0, 48)
    colok = (kc_[None, :] >= cs[:, None]) & (kc_[None, :] < cs[:, None] + 16)
    neg = np.zeros((128, 9, 128), np.float32)
    for ci, (delta, typ) in enumerate(NA_COMBOS):
        for krl in range(2):
            for qrl in range(2):
                if typ == "F":
                    rowok = True
                elif typ == "M-":
                    rowok = not (krl == 0 and qrl == 1)
                else:
                    rowok = (krl == 0 and qrl == 1)
                blk = colok.T if rowok else np.zeros((64, 64), bool)
                neg[krl * 64:(krl + 1) * 64, ci, qrl * 64:(qrl + 1) * 64] = np.where(blk, 0.0, -30000.0)
    c["negmask9"] = neg
    return c


def na_bias_layout(rpb):
    depth = rpb.shape[0]
    kcq = np.arange(64)
    coloff = np.clip(kcq[:, None] - kcq[None, :], -15, 15) + 15
    out = np.zeros((depth, 8, 128, 9, 128), np.float32)
    for ci, (delta, typ) in enumerate(NA_COMBOS):
        for krl in range(2):
            for qrl in range(2):
                ro = delta + krl - qrl + 7
                if ro < 0 or ro > 14:
                    continue
                out[:, :, krl * 64:(krl + 1) * 64, ci, qrl * 64:(qrl + 1) * 64] = rpb[:, :, ro][:, :, coloff]
    return out


def build_program(N, depth, debug=False, stop_after=None):
    L = N + NMETA
    NT = N // 128
    NG = N // 512
    ROWS = N // 64
    NAP = na_plan(ROWS)
    nc = bass.Bass("TRN2", target_bir_lowering=False)

    def din(name, shape, dt=F32):
        return nc.dram_tensor(name, shape, dt, kind="ExternalInput").ap()

    def dscr(name, shape, dt):
        return nc.dram_tensor(name, shape, dt, kind=("ExternalOutput" if (debug and name in debug) else "Internal")).ap()

    x = din("x", [N, D])
    meta = din("meta_tokens", [NMETA, D])
    nm_cols = din("nm_cols", [depth, 128, KC])
    nf_cols = din("nf_cols", [depth, 128, KC])
    nfin = din("norm_final", [D])
    w_in = din("w_in", [depth, D, IN_W])
    w_gate = din("w_gate", [depth, 3, D, D])
    w_branch = din("w_branch", [depth, 3, 1024, D])
    w_out = din("w_out", [depth, D, D])
    w_fin = din("w_ffn_in", [depth, D, 2 * FFN])
    w_fout = din("w_ffn_out", [depth, FFN, D])
    na_b9 = din("na_bias9", [depth, 8, 128, 9, 128])
    dlam = din("diff_lambda", [depth, 256])
    subln = din("subln_col", [depth, 128, 1])
    sink = din("gqa_sink", [depth, 8])
    c_ident = din("ident", [128, 128])
    c_pmB = din("pmB", [128, 128])
    c_pmC = din("pmC", [128, 128])
    c_cosB = din("cosB", [128, L]); c_sinB = din("sinB", [128, L])
    c_cosC = din("cosC", [128, L]); c_sinC = din("sinC", [128, L])
    c_mprev4 = din("mprev4", [128, 4, 128]); c_mnext4 = din("mnext4", [128, 4, 128])
    c_near4 = din("near4", [128, 4, 16])
    c_neg9 = din("negmask9", [128, 9, 128])
    y = nc.dram_tensor("y", [N, D], F32, kind="ExternalOutput").ap()

    wb_in = dscr("wb_in", [depth, D, IN_W], BF16)
    wb_gate = dscr("wb_gate", [depth, 3, D, D], BF16)
    wb_branch = dscr("wb_branch", [depth, 3, 1024, D], BF16)
    wb_out = dscr("wb_out", [depth, D, D], BF16)
    wb_fin = dscr("wb_fin", [depth, D, 2 * FFN], BF16)
    wb_fout = dscr("wb_fout", [depth, FFN, D], BF16)
    wk_in = dscr("wk_in", [depth, 15, 128, KC * 512], BF16)
    wk_gate = dscr("wk_gate", [depth, 3, 4, 128, KC * 512], BF16)
    wk_branch = dscr("wk_branch", [depth, 3, 4, 128, 8 * 512], BF16)
    wk_out = dscr("wk_out", [depth, 4, 128, KC * 512], BF16)
    wk_fin = dscr("wk_fin", [depth, 44, 128, KC * 256], BF16)
    wk_fout = dscr("wk_fout", [depth, 16, 128, 11 * 512], BF16)
    QaT = dscr("QaT", [1024, L], BF16); KaT = dscr("KaT", [1024, L], BF16); Va = dscr("Va", [L, 1024], BF16)
    QbT = dscr("QbT", [1024, L], BF16); KbT = dscr("KbT", [1024, L], BF16); Vb = dscr("Vb", [L, 1024], BF16)
    QcT = dscr("QcT", [1024, L], BF16); KcT = dscr("KcT", [256, L], BF16); Vc = dscr("Vc", [L, 256], BF16)
    OT = [dscr("OaT", [1024, L], BF16), dscr("ObT", [1024, L], BF16), dscr("OcT", [1024, L], BF16)]
    h1 = dscr("h1", [L, D], F32)

    P = Prog()
    build_program.last_prog = P
    import contextlib
    es = contextlib.ExitStack()
    arena_t = es.enter_context(nc.sbuf_tensor("arena", [128, ARENA_BYTES], U8))
    ps = es.enter_context(nc.psum_tensor("ps", [128, 8, 512], F32))
    A = Arena(arena_t, ARENA_BYTES)
    psb = [Buf("psum%d" % i) for i in range(8)]
    bank_rr = [0]

    def next_bank(lo=0, hi=8):
        b = lo + bank_rr[0] % (hi - lo)
        bank_rr[0] += 1
        return b

    dq_rr = [0]

    def dq():
        dq_rr[0] += 1
        return "sp" if dq_rr[0] % 2 else "pool"

    def dma(q, out, in_, reads=(), writes=()):
        return P.add(q, lambda e: e.dma_start(out=out, in_=in_), reads=reads, writes=writes, dma=True)

    ident, ident_b = A.alloc([128], F32, name="ident")
    pmB, pmB_b = A.alloc([128], F32, name="pmB")
    pmC, pmC_b = A.alloc([128], F32, name="pmC")
    ones, ones_b = A.alloc([128], BF16, name="ones")
    gfin, gfin_b = A.alloc([D], F32, name="gfin")
    smalls, smalls_b = A.alloc([64], F32, name="smalls")
    dma("sp", ident, c_ident, writes=[ident_b])
    dma("sp", pmB, c_pmB, writes=[pmB_b])
    dma("sp", pmC, c_pmC, writes=[pmC_b])
    dma("sp", gfin, nfin.partition_broadcast(128), writes=[gfin_b])
    P.add("dve", lambda e: e.memset(ones, 1.0), writes=[ones_b])
    onesf, onesf_b = A.alloc([128], F32, name="onesf")
    P.add("dve", lambda e: e.memset(onesf, 1.0), writes=[onesf_b])
    PERSIST = A.off

    def cast(src, dst, rows, step=128):
        for r0 in range(0, rows, step):
            dma("pool", dst[r0:r0 + step, :], src[r0:r0 + step, :])
    for l in range(depth):
        cast(w_in[l], wb_in[l], D)
        for i in range(3):
            cast(w_gate[l, i], wb_gate[l, i], D)
            cast(w_branch[l, i], wb_branch[l, i], 1024)
        cast(w_out[l], wb_out[l], D)
        cast(w_fin[l], wb_fin[l], D)
        cast(w_fout[l], wb_fout[l], FFN)
    P.barrier()

    def conv(src, dst, nk, ncol):
        dma("sp", dst.rearrange("p (k c) -> p k c", k=nk), src.rearrange("(k p) c -> p k c", p=128))
    for l in range(depth):
        for blk in range(15):
            conv(wb_in[l][:, blk * 512:(blk + 1) * 512], wk_in[l, blk], KC, 512)
        for i in range(3):
            for cb in range(4):
                conv(wb_gate[l, i][:, cb * 512:(cb + 1) * 512], wk_gate[l, i, cb], KC, 512)
                conv(wb_branch[l, i][:, cb * 512:(cb + 1) * 512], wk_branch[l, i, cb], 8, 512)
        for nb in range(4):
            conv(wb_out[l][:, nb * 512:(nb + 1) * 512], wk_out[l, nb], KC, 512)
        for cb in range(44):
            conv(wb_fin[l][:, cb * 256:(cb + 1) * 256], wk_fin[l, cb], KC, 256)
        for hp in range(4):
            for nb in range(4):
                r0 = hp * 1408
                conv(wb_fout[l][r0:r0 + 1408, nb * 512:(nb + 1) * 512], wk_fout[l, hp * 4 + nb], 11, 512)
    P.barrier()

    groups = [(512 * g, [(512 * g + 128 * t, 128) for t in range(4)]) for g in range(NG)]
    groups.append((N, [(N, NMETA)]))

    def h_src(l, col, nt):
        if l == 0:
            return x[col:col + nt, :] if col < N else meta[0:nt, :]
        return h1[col:col + nt, :]

    class WS:
        def __init__(self, nslot, slot_bytes):
            self.slots = []
            for i in range(nslot):
                v, b = A.alloc([slot_bytes // 2], BF16, name="wslot%d" % i)
                self.slots.append((v, b))
            self.i = 0

        def load(self, src, nk, ncol):
            v, b = self.slots[self.i % len(self.slots)]
            self.i += 1
            dma("sp", v[:, 0:nk * ncol], src, writes=[b])
            return v[:, 0:nk * ncol].rearrange("p (k c) -> p k c", k=nk), b

    def rmsnorm_T(ht, ht_b, nt, gcols, gcols_b, dstT, dstT_b, c0, tmp):
        junk, junk_b, xn, xn_b, st, st_b = tmp
        P.add("act", lambda e: e.activation(out=junk[0:nt, :], in_=ht[0:nt, :], func=AF.Square, accum_out=st[0:nt, 0:1]),
              reads=[ht_b], writes=[junk_b, st_b])
        P.add("dve", lambda e: e.tensor_scalar(out=st[0:nt, 1:2], in0=st[0:nt, 0:1], scalar1=1.0 / D, scalar2=EPS,
                                               op0=ALU.mult, op1=ALU.add), reads=[st_b], writes=[st_b])
        P.add("act", lambda e: e.activation(out=st[0:nt, 2:3], in_=st[0:nt, 1:2], func=AF.Sqrt), reads=[st_b], writes=[st_b])
        P.add("dve", lambda e: e.reciprocal(out=st[0:nt, 3:4], in_=st[0:nt, 2:3]), reads=[st_b], writes=[st_b])
        P.add("dve", lambda e: e.tensor_scalar(out=xn[0:nt, :], in0=ht[0:nt, :], scalar1=st[0:nt, 3:4], scalar2=None,
                                               op0=ALU.mult), reads=[ht_b, st_b], writes=[xn_b])
        for k4 in range(4):
            bk = next_bank()
            for j in range(4):
                kc = k4 * 4 + j
                P.add("pe", lambda e, kc=kc, j=j, bk=bk: e.transpose(out=ps[:, bk, j * 128:j * 128 + nt],
                                                                    in_=xn[0:nt, kc * 128:(kc + 1) * 128],
                                                                    identity=ident[0:nt, 0:nt]),
                      reads=[xn_b, ident_b], writes=[psb[bk]])
            for j in range(4):
                kc = k4 * 4 + j
                eng = "act" if j % 2 else "dve"
                if eng == "dve":
                    P.add("dve", lambda e, kc=kc, j=j, bk=bk: e.tensor_scalar(
                        out=dstT[:, kc, c0:c0 + nt], in0=ps[:, bk, j * 128:j * 128 + nt],
                        scalar1=gcols[:, kc:kc + 1], scalar2=None, op0=ALU.mult),
                        reads=[psb[bk], gcols_b], writes=[dstT_b])
                else:
                    P.add("act", lambda e, kc=kc, j=j, bk=bk: e.activation(
                        out=dstT[:, kc, c0:c0 + nt], in_=ps[:, bk, j * 128:j * 128 + nt],
                        func=AF.Copy, scale=gcols[:, kc:kc + 1]),
                        reads=[psb[bk], gcols_b], writes=[dstT_b])

    Va_img = dscr("Va_img", [8, 128, NT + 1, 128], BF16)
    Vb_img = dscr("Vb_img", [8, 128, NT + 1, 128], BF16)
    Vc_img = dscr("Vc_img", [2, 128, NT + 1, 128], BF16)
    y_bufs = []

    for l in range(depth):
        last = (l == depth - 1)
        lam_init = 0.8 - 0.6 * float(np.exp(-0.3 * l))
        A.reset(PERSIST)
        ws = WS(3, 16 * 1024)
        gmix, gmix_b = A.alloc([KC], F32, name="gmix")
        dma("sp", gmix, nm_cols[l], writes=[gmix_b])
        uT, uT_b = A.alloc([KC, 512], BF16, name="uT")
        hts = [A.alloc([D], F32, name="ht%d" % i) for i in range(2)]
        junk, junk_b = A.alloc([D], F32, name="junk")
        xn, xn_b = A.alloc([D], F32, name="xn")
        st, st_b = A.alloc([8], F32, name="st")
        tmpn = (junk, junk_b, xn, xn_b, st, st_b)
        ropeT = [A.alloc([512], F32, name="rope%d" % i) for i in range(4)]
        qfs = [A.alloc([512], F32, name="qf%d" % i) for i in range(2)]
        t1s = [A.alloc([512], F32, name="t1%d" % i) for i in range(2)]
        t2s = [A.alloc([512], F32, name="t2%d" % i) for i in range(2)]
        obs = [A.alloc([512], BF16, name="ob%d" % i) for i in range(4)]
        ob_i = 0
        rp_i = 0
        for gidx, (c0, tiles) in enumerate(groups):
            if gidx > 0:
                P.barrier()
            ntok = sum(nt for _, nt in tiles)
            for ti, (col, nt) in enumerate(tiles):
                ht, ht_b = hts[ti % 2]
                dma(dq(), ht[0:nt, :], h_src(l, col, nt), writes=[ht_b])
                rmsnorm_T(ht, ht_b, nt, gmix, gmix_b, uT, uT_b, col - c0, tmpn)
            for (tb, tb_b), src in zip(ropeT, (c_cosB, c_sinB, c_cosC, c_sinC)):
                dma(dq(), tb[:, 0:ntok], src[:, c0:c0 + ntok], writes=[tb_b])
            for blk in range(15):
                wv, wv_b = ws.load(wk_in[l, blk], KC, 512)
                vinfo = None
                if blk in (4, 5):
                    vinfo = (0, 512, Va_img, (blk - 4) * 4, 4)
                elif blk in (10, 11):
                    vinfo = (0, 512, Vb_img, (blk - 10) * 4, 4)
                elif blk == 14:
                    vinfo = (256, 256, Vc_img, 0, 2)
                if vinfo is not None:
                    vc0, vnc, vimg, h0, nh = vinfo
                    for ti, (col, nt) in enumerate(tiles):
                        bk = next_bank()
                        tc0 = col - c0
                        for kc in range(KC):
                            P.add("pe", lambda e, kc=kc, bk=bk, tc0=tc0, nt=nt: e.matmul(
                                ps[0:nt, bk, 0:vnc], uT[:, kc, tc0:tc0 + nt], wv[:, kc, vc0:vc0 + vnc],
                                start=(kc == 0), stop=(kc == KC - 1)),
                                reads=[uT_b, wv_b], writes=[psb[bk]])
                        ob, ob_b = obs[ob_i % 4]
                        ob_i += 1
                        P.add("act", lambda e, bk=bk, ob=ob, nt=nt: e.activation(out=ob[0:nt, 0:vnc], in_=ps[0:nt, bk, 0:vnc], func=AF.Copy),
                              reads=[psb[bk]], writes=[ob_b])
                        tix = (col // 128) if col < N else NT
                        dma(dq(), vimg[h0:h0 + nh, 0:nt, tix, :].rearrange("h p d -> p h d"),
                            ob[0:nt, 0:vnc].rearrange("p (h d) -> p h d", h=nh), reads=[ob_b])
                if blk in (4, 5, 10, 11):
                    continue
                for j in range(2 if blk == 14 else 4):
                    chunk = blk * 4 + j
                    if chunk < 8:
                        kind, dst, r0 = "plain", QaT, chunk * 128
                    elif chunk < 16:
                        kind, dst, r0 = "plain", KaT, (chunk - 8) * 128
                    elif chunk < 32:
                        kind, dst, r0 = "ropeB", QbT, (chunk - 24) * 128
                    elif chunk < 40:
                        kind, dst, r0 = "ropeB", KbT, (chunk - 32) * 128
                    elif chunk < 56:
                        kind, dst, r0 = "ropeC", QcT, (chunk - 48) * 128
                    else:
                        kind, dst, r0 = "ropeC", KcT, (chunk - 56) * 128
                    bk = next_bank()
                    for kc in range(KC):
                        P.add("pe", lambda e, kc=kc, bk=bk, j=j: e.matmul(
                            ps[:, bk, 0:ntok], wv[:, kc, j * 128:(j + 1) * 128], uT[:, kc, 0:ntok],
                            start=(kc == 0), stop=(kc == KC - 1)),
                            reads=[uT_b, wv_b], writes=[psb[bk]])
                    ob, ob_b = obs[ob_i % 4]
                    ob_i += 1
                    if kind == "plain":
                        P.add("act", lambda e, bk=bk, ob=ob: e.activation(out=ob[:, 0:ntok], in_=ps[:, bk, 0:ntok], func=AF.Copy),
                              reads=[psb[bk]], writes=[ob_b])
                    else:
                        qf, qf_b = qfs[rp_i % 2]
                        t1, t1_b = t1s[rp_i % 2]
                        t2, t2_b = t2s[rp_i % 2]
                        rp_i += 1
                        if kind == "ropeB":
                            pm, pm_b, (ct, ct_b), (sn, sn_b) = pmB, pmB_b, ropeT[0], ropeT[1]
                        else:
                            pm, pm_b, (ct, ct_b), (sn, sn_b) = pmC, pmC_b, ropeT[2], ropeT[3]
                        P.add("act", lambda e, bk=bk, qf=qf: e.activation(out=qf[:, 0:ntok], in_=ps[:, bk, 0:ntok], func=AF.Copy),
                              reads=[psb[bk]], writes=[qf_b])
                        bk2 = next_bank()
                        P.add("pe", lambda e, bk2=bk2, pm=pm, qf=qf: e.matmul(ps[:, bk2, 0:ntok], pm, qf[:, 0:ntok], start=True, stop=True),
                              reads=[pm_b, qf_b], writes=[psb[bk2]])
                        P.add("pool", lambda e, t1=t1, qf=qf, ct=ct: e.tensor_tensor(out=t1[:, 0:ntok], in0=qf[:, 0:ntok], in1=ct[:, 0:ntok], op=ALU.mult),
                              reads=[qf_b, ct_b], writes=[t1_b])
                        P.add("dve", lambda e, t2=t2, bk2=bk2, sn=sn: e.tensor_tensor(out=t2[:, 0:ntok], in0=ps[:, bk2, 0:ntok], in1=sn[:, 0:ntok], op=ALU.mult),
                              reads=[psb[bk2], sn_b], writes=[t2_b])
                        P.add("dve", lambda e, ob=ob, t1=t1, t2=t2: e.tensor_tensor(out=ob[:, 0:ntok], in0=t1[:, 0:ntok], in1=t2[:, 0:ntok], op=ALU.add),
                              reads=[t1_b, t2_b], writes=[ob_b])
                    dma(dq(), dst[r0:r0 + 128, c0:c0 + ntok], ob[:, 0:ntok], reads=[ob_b])
        P.barrier()
        if stop_after == "p1":
            break
        A.reset(PERSIST)
        lamt, lamt_b = A.alloc([256], F32, name="lamt")
        sm, sm_b = A.alloc([16], F32, name="sm")
        sinkt, sinkt_b = A.alloc([8], F32, name="sinkt")
        gsub, gsub_b = A.alloc([2], F32, name="gsub")
        neg9, neg9_b = A.alloc([9, 128], F32, name="neg9")
        mprev, mprev_b = A.alloc([512], BF16, name="mprev")
        mnext, mnext_b = A.alloc([512], BF16, name="mnext")
        near, near_b = A.alloc([64], BF16, name="near")
        dma("sp", lamt, dlam[l].partition_broadcast(128), writes=[lamt_b])
        dma("sp", sinkt, sink[l].partition_broadcast(128), writes=[sinkt_b])
        dma("sp", gsub[:, 0:1], subln[l], writes=[gsub_b])
        dma("sp", neg9, c_neg9, writes=[neg9_b])
        dma("pool", mprev, c_mprev4.rearrange("p a b -> p (a b)"), writes=[mprev_b])
        dma("pool", mnext, c_mnext4.rearrange("p a b -> p (a b)"), writes=[mnext_b])
        dma("pool", near, c_near4.rearrange("p a b -> p (a b)"), writes=[near_b])
        prodt, prodt_b = A.alloc([128], F32, name="prodt")
        P.add("dve", lambda e: e.tensor_tensor(out=prodt[:, 0:64], in0=lamt[:, 0:64], in1=lamt[:, 64:128], op=ALU.mult), reads=[lamt_b], writes=[prodt_b])
        P.add("dve", lambda e: e.tensor_tensor(out=prodt[:, 64:128], in0=lamt[:, 128:192], in1=lamt[:, 192:256], op=ALU.mult), reads=[lamt_b], writes=[prodt_b])
        P.add("dve", lambda e: e.tensor_reduce(out=sm[:, 0:2], in_=prodt.rearrange("p (a b) -> p a b", a=2), axis=mybir.AxisListType.X, op=ALU.add), reads=[prodt_b], writes=[sm_b])
        P.add("act", lambda e: e.activation(out=sm[:, 2:4], in_=sm[:, 0:2], func=AF.Exp), reads=[sm_b], writes=[sm_b])
        P.add("dve", lambda e: e.tensor_tensor(out=sm[:, 4:5], in0=sm[:, 3:4], in1=sm[:, 2:3], op=ALU.subtract), reads=[sm_b], writes=[sm_b])
        P.add("dve", lambda e: e.tensor_scalar(out=sm[:, 5:6], in0=sm[:, 4:5], scalar1=-lam_init, scalar2=None, op0=ALU.add), reads=[sm_b], writes=[sm_b])
        neglam = sm[:, 5:6]
        P.add("dve", lambda e: e.tensor_scalar(out=gsub[:, 1:2], in0=gsub[:, 0:1], scalar1=(1.0 - lam_init), scalar2=None, op0=ALU.mult), reads=[gsub_b], writes=[gsub_b])
        gsubs = gsub[:, 1:2]
        expsink, expsink_b = A.alloc([8], F32, name="expsink")
        P.add("act", lambda e: e.activation(out=expsink, in_=sinkt, func=AF.Exp), reads=[sinkt_b], writes=[expsink_b])

        sets = []
        for i in range(2):
            sets.append(dict(
                QT=A.alloc([L], BF16, name="QT%d" % i), KT=A.alloc([L], BF16, name="KT%d" % i),
                V=A.alloc([NT + 1, 128], BF16, name="Vh%d" % i), O=A.alloc([L], BF16, name="Oh%d" % i),
                bm=A.alloc([9, 128], F32, name="bm%d" % i)))
        tmpfs = [A.alloc([5, 128], F32, name="tmpf%d" % i) for i in range(2)]
        PTs = [A.alloc([5, 128], BF16, name="PT%d" % i) for i in range(2)]
        PTms = [A.alloc([128], BF16, name="PTm%d" % i) for i in range(2)]
        rcs = [A.alloc([128], F32, name="rc%d" % i) for i in range(2)]
        PTd = [A.alloc([2, 512], BF16, name="PTd%d" % i) for i in range(3)]
        dacc = [A.alloc([512], F32, name="dacc%d" % i) for i in range(2)]
        dR = [A.alloc([512], F32, name="dR%d" % i) for i in range(2)]
        dT = [A.alloc([512], F32, name="dT%d" % i) for i in range(2)]
        dod, dod_b = A.alloc([512], F32, name="dod")
        dsq, dsq_b = A.alloc([512], BF16, name="dsq")
        dri, dri_b = A.alloc([512], F32, name="dri")
        Qg = [A.alloc([4, 128], BF16, name="Qg%d" % i) for i in range(2)]
        PTg = [A.alloc([512], BF16, name="PTg%d" % i) for i in range(4)]
        gdn, gdn_b = A.alloc([512], F32, name="gdn")
        gout = [A.alloc([512], BF16, name="gout%d" % i) for i in range(2)]
        set_i = 0
        SC = 128 ** -0.5
        SCB = 64 ** -0.5

        def load_head(S, qsrc, ksrc, vimg_h):
            if qsrc is not None:
                dma(dq(), S["QT"][0], qsrc, writes=[S["QT"][1]])
            dma(dq(), S["KT"][0], ksrc, writes=[S["KT"][1]])
            dma(dq(), S["V"][0], vimg_h, writes=[S["V"][1]])

        unit = 0
        for h in range(8):
            if h % 2 == 0 and h > 0:
                P.barrier()
            S = sets[set_i % 2]; set_i += 1
            QT, QT_b = S["QT"]; KT, KT_b = S["KT"]; Vh, Vh_b = S["V"]; Oh, Oh_b = S["O"]; bm, bm_b = S["bm"]
            load_head(S, QaT[h * 128:(h + 1) * 128, :], KaT[h * 128:(h + 1) * 128, :], Va_img[h])
            dma(dq(), bm, na_b9[l, h], writes=[bm_b])
            P.add("dve", lambda e, bm=bm: e.tensor_tensor(out=bm, in0=bm, in1=neg9, op=ALU.add), reads=[bm_b, neg9_b], writes=[bm_b])
            units = [(rp * 128, 128, klist) for rp, klist in enumerate(NAP)]
            if not last:
                units.append((N, NMETA, None))
            for (q0, nq, klist) in units:
                u2 = unit % 2; unit += 1
                SA, SB, OB, DB = 4 * u2, 4 * u2 + 1, 4 * u2 + 2, 4 * u2 + 3
                tmpf, tmpf_b = tmpfs[u2]; PT, PT_b = PTs[u2]; PTm, PTm_b = PTms[u2]; rc, rc_b = rcs[u2]
                nk_t = len(klist) if klist is not None else 0
                for j in range(nk_t):
                    kp, ci = klist[j]
                    bk, off = (SA, j * 128) if j < 4 else (SB, 0)
                    P.add("pe", lambda e, bk=bk, off=off, kp=kp: e.matmul(ps[:, bk, off:off + nq], KT[:, kp * 128:(kp + 1) * 128], QT[:, q0:q0 + nq], start=True, stop=True),
                          reads=[KT_b, QT_b], writes=[psb[bk]])
                P.add("pe", lambda e: e.matmul(ps[0:NMETA, SB, 128:128 + nq], KT[:, N:N + NMETA], QT[:, q0:q0 + nq], start=True, stop=True),
                      reads=[KT_b, QT_b], writes=[psb[SB]])
                for j in range(nk_t):
                    kp, ci = klist[j]
                    bk, off = (SA, j * 128) if j < 4 else (SB, 0)
                    P.add("dve", lambda e, bk=bk, off=off, j=j, ci=ci: e.scalar_tensor_tensor(out=tmpf[:, j, 0:nq], in0=ps[:, bk, off:off + nq], scalar=SC, in1=bm[:, ci, 0:nq], op0=ALU.mult, op1=ALU.add),
                          reads=[psb[bk], bm_b], writes=[tmpf_b])
                if nk_t:
                    P.add("act", lambda e: e.activation(out=PT[:, 0:nk_t, :], in_=tmpf[:, 0:nk_t, :], func=AF.Exp), reads=[tmpf_b], writes=[PT_b])
                P.add("act", lambda e: e.activation(out=PTm[0:NMETA, 0:nq], in_=ps[0:NMETA, SB, 128:128 + nq], func=AF.Exp, scale=SC), reads=[psb[SB]], writes=[PTm_b])
                for j in range(nk_t):
                    kp, ci = klist[j]
                    P.add("pe", lambda e, j=j, kp=kp: e.matmul(ps[:, OB, 0:nq], Vh[:, kp, :], PT[:, j, 0:nq], start=(j == 0), stop=False), reads=[Vh_b, PT_b], writes=[psb[OB]])
                    P.add("pe", lambda e, j=j: e.matmul(ps[:, DB, 0:nq], ones, PT[:, j, 0:nq], start=(j == 0), stop=False), reads=[ones_b, PT_b], writes=[psb[DB]])
                P.add("pe", lambda e: e.matmul(ps[:, OB, 0:nq], Vh[0:NMETA, NT, :], PTm[0:NMETA, 0:nq], start=(nk_t == 0), stop=True), reads=[Vh_b, PTm_b], writes=[psb[OB]])
                P.add("pe", lambda e: e.matmul(ps[:, DB, 0:nq], ones[0:NMETA, :], PTm[0:NMETA, 0:nq], start=(nk_t == 0), stop=True), reads=[ones_b, PTm_b], writes=[psb[DB]])
                P.add("dve", lambda e: e.reciprocal(out=rc[:, 0:nq], in_=ps[:, DB, 0:nq]), reads=[psb[DB]], writes=[rc_b])
                P.add("dve", lambda e: e.tensor_tensor(out=Oh[:, q0:q0 + nq], in0=ps[:, OB, 0:nq], in1=rc[:, 0:nq], op=ALU.mult), reads=[psb[OB], rc_b], writes=[Oh_b])
            LO = L if not last else N
            dma(dq(), OT[0][h * 128:(h + 1) * 128, 0:LO], Oh[:, 0:LO], reads=[Oh_b])

        qranges = [(512 * g, 512) for g in range(NG)]
        if not last:
            qranges.append((N, NMETA))
        pt_i = 0
        for h in range(8):
            P.barrier()
            S = sets[set_i % 2]; set_i += 1
            QT, QT_b = S["QT"]; KT, KT_b = S["KT"]; Vh, Vh_b = S["V"]; Oh, Oh_b = S["O"]
            load_head(S, QbT[h * 128:(h + 1) * 128, :], KbT[h * 128:(h + 1) * 128, :], Vb_img[h])
            for (q0, nq) in qranges:
                ktiles = [(kt * 128, 128, kt) for kt in range(NT)] + [(N, NMETA, NT)]
                nkt = len(ktiles)

                def qk(i):
                    kcol, nk, kt = ktiles[i]
                    for c in range(2):
                        bk = 4 + (2 * i + c) % 4
                        P.add("pe", lambda e, bk=bk, c=c, kcol=kcol, nk=nk: e.matmul(ps[0:nk, bk, 0:nq], KT[c * 64:(c + 1) * 64, kcol:kcol + nk], QT[c * 64:(c + 1) * 64, q0:q0 + nq], start=True, stop=True),
                              reads=[KT_b, QT_b], writes=[psb[bk]])

                def pv(i, pt_i):
                    kcol, nk, kt = ktiles[i]
                    bk0 = 4 + (2 * i) % 4
                    pt, pt_b = PTd[pt_i % 3]
                    P.add("act", lambda e: e.activation(out=pt[0:nk, :, 0:nq], in_=ps[0:nk, bk0:bk0 + 2, 0:nq], func=AF.Exp, scale=SCB),
                          reads=[psb[bk0], psb[bk0 + 1]], writes=[pt_b])
                    for c in range(2):
                        P.add("pe", lambda e: e.matmul(ps[:, c, 0:nq], Vh[0:nk, kt, :], pt[0:nk, c, 0:nq], start=(i == 0), stop=(i == nkt - 1)), reads=[Vh_b, pt_b], writes=[psb[c]])
                        da, da_b = dacc[c]
                        eng = "pool" if c == 0 else "dve"
                        if i == 0:
                            P.add(eng, lambda e: e.tensor_copy(out=da[0:nk, 0:nq], in_=pt[0:nk, c, 0:nq]), reads=[pt_b], writes=[da_b])
                        else:
                            P.add(eng, lambda e: e.tensor_tensor(out=da[0:nk, 0:nq], in0=da[0:nk, 0:nq], in1=pt[0:nk, c, 0:nq], op=ALU.add), reads=[pt_b, da_b], writes=[da_b])

                qk(0)
                for i in range(nkt):
                    if i + 1 < nkt:
                        qk(i + 1)
                    pv(i, pt_i)
                    pt_i += 1
                for c in range(2):
                    P.add("pe", lambda e: e.matmul(ps[:, 2 + c, 0:nq], onesf, dacc[c][0][:, 0:nq], start=True, stop=True), reads=[onesf_b, dacc[c][1]], writes=[psb[2 + c]])
                for c in range(2):
                    P.add("dve", lambda e, c=c: e.reciprocal(out=dR[c][0][:, 0:nq], in_=ps[:, 2 + c, 0:nq]), reads=[psb[2 + c]], writes=[dR[c][1]])
                    P.add("dve", lambda e, c=c: e.tensor_tensor(out=dT[c][0][:, 0:nq], in0=ps[:, c, 0:nq], in1=dR[c][0][:, 0:nq], op=ALU.mult), reads=[psb[c], dR[c][1]], writes=[dT[c][1]])
                P.add("dve", lambda e: e.scalar_tensor_tensor(out=dod[:, 0:nq], in0=dT[1][0][:, 0:nq], scalar=neglam, in1=dT[0][0][:, 0:nq], op0=ALU.mult, op1=ALU.add),
                      reads=[dT[0][1], dT[1][1], sm_b], writes=[dod_b])
                P.add("pool", lambda e: e.tensor_tensor(out=dsq[:, 0:nq], in0=dod[:, 0:nq], in1=dod[:, 0:nq], op=ALU.mult), reads=[dod_b], writes=[dsq_b])
                P.add("pe", lambda e: e.matmul(ps[:, 4, 0:nq], ones, dsq[:, 0:nq], start=True, stop=True), reads=[ones_b, dsq_b], writes=[psb[4]])
                P.add("dve", lambda e: e.tensor_scalar(out=dri[:, 0:nq], in0=ps[:, 4, 0:nq], scalar1=1.0 / 128, scalar2=EPS, op0=ALU.mult, op1=ALU.add), reads=[psb[4]], writes=[dri_b])
                P.add("act", lambda e: e.activation(out=dri[:, 0:nq], in_=dri[:, 0:nq], func=AF.Sqrt), reads=[dri_b], writes=[dri_b])
                P.add("dve", lambda e: e.reciprocal(out=dri[:, 0:nq], in_=dri[:, 0:nq]), reads=[dri_b], writes=[dri_b])
                P.add("dve", lambda e: e.scalar_tensor_tensor(out=Oh[:, q0:q0 + nq], in0=dod[:, 0:nq], scalar=gsubs, in1=dri[:, 0:nq], op0=ALU.mult, op1=ALU.mult),
                      reads=[dod_b, dri_b, gsub_b], writes=[Oh_b])
            LO = L if not last else N
            dma(dq(), OT[1][h * 128:(h + 1) * 128, 0:LO], Oh[:, 0:LO], reads=[Oh_b])

        gi = 0
        P.barrier()
        for g in range(2):
            S = sets[set_i % 2]; set_i += 1
            KT, KT_b = S["KT"]; Vh, Vh_b = S["V"]
            load_head(S, None, KcT[g * 128:(g + 1) * 128, :], Vc_img[g])
            gunits = []
            for b in range(NT):
                tl = []
                if b > 0:
                    tl.append(((b - 1) * 128, 128, b - 1, (mprev, mprev_b)))
                tl.append((b * 128, 128, b, None))
                if b < NT - 1:
                    tl.append(((b + 1) * 128, 128, b + 1, (mnext, mnext_b)))
                tl.append((N, NMETA, NT, None))
                gunits.append((b * 128, 128, tl))
            if not last:
                gunits.append((N, NMETA, [(N, NMETA, NT, None), (0, 128, 0, (near, near_b))]))
            for (q0, nq, tl) in gunits:
                Q4, Q4_b = Qg[gi % 2]
                go, go_b = gout[gi % 2]
                gi += 1
                w4 = 4 * nq
                q4v = Q4.rearrange("p a b -> p (a b)")[:, 0:w4]
                dma(dq(), q4v.rearrange("p (a b) -> p a b", a=4), QcT[g * 512:(g + 1) * 512, q0:q0 + nq].rearrange("(r d) t -> d r t", d=128), writes=[Q4_b])
                for j, (kcol, nk, kt, msk) in enumerate(tl):
                    P.add("pe", lambda e, j=j, kcol=kcol, nk=nk: e.matmul(ps[0:nk, j, 0:w4], KT[:, kcol:kcol + nk], q4v, start=True, stop=True), reads=[KT_b, Q4_b], writes=[psb[j]])
                for j, (kcol, nk, kt, msk) in enumerate(tl):
                    pt, pt_b = PTg[j]
                    P.add("act", lambda e, j=j, nk=nk, pt=pt: e.activation(out=pt[0:nk, 0:w4], in_=ps[0:nk, j, 0:w4], func=AF.Exp, scale=SC), reads=[psb[j]], writes=[pt_b])
                    if msk is not None:
                        P.add("pool", lambda e, pt=pt, nk=nk, msk=msk: e.tensor_tensor(out=pt[0:nk, 0:w4], in0=pt[0:nk, 0:w4], in1=msk[0][0:nk, 0:w4], op=ALU.mult), reads=[pt_b, msk[1]], writes=[pt_b])
                for j, (kcol, nk, kt, msk) in enumerate(tl):
                    pt, pt_b = PTg[j]
                    P.add("pe", lambda e, j=j, nk=nk, kt=kt, pt=pt: e.matmul(ps[:, 4, 0:w4], Vh[0:nk, kt, :], pt[0:nk, 0:w4], start=(j == 0), stop=(j == len(tl) - 1)), reads=[Vh_b, pt_b], writes=[psb[4]])
                    P.add("pe", lambda e, j=j, nk=nk, pt=pt: e.matmul(ps[:, 5, 0:w4], ones[0:nk, :], pt[0:nk, 0:w4], start=(j == 0), stop=(j == len(tl) - 1)), reads=[ones_b, pt_b], writes=[psb[5]])
                for r in range(4):
                    P.add("dve", lambda e, r=r: e.tensor_scalar(out=gdn[:, r * nq:(r + 1) * nq], in0=ps[:, 5, r * nq:(r + 1) * nq], scalar1=expsink[:, g * 4 + r:g * 4 + r + 1], scalar2=None, op0=ALU.add),
                          reads=[psb[5], expsink_b], writes=[gdn_b])
                P.add("dve", lambda e: e.reciprocal(out=gdn[:, 0:w4], in_=gdn[:, 0:w4]), reads=[gdn_b], writes=[gdn_b])
                P.add("dve", lambda e, go=go: e.tensor_tensor(out=go[:, 0:w4], in0=ps[:, 4, 0:w4], in1=gdn[:, 0:w4], op=ALU.mult), reads=[psb[4], gdn_b], writes=[go_b])
                dma(dq(), OT[2][g * 512:(g + 1) * 512, q0:q0 + nq].rearrange("(r d) t -> d r t", d=128), go[:, 0:w4].rearrange("p (a b) -> p a b", a=4), reads=[go_b])
        P.barrier()
        if stop_after == "p2":
            break
        A.reset(PERSIST)
        ws = WS(3, 16 * 1024)
        gmix, gmix_b = A.alloc([KC], F32, name="gmix3")
        gffn, gffn_b = A.alloc([KC], F32, name="gffn3")
        dma("sp", gmix, nm_cols[l], writes=[gmix_b])
        dma("sp", gffn, nf_cols[l], writes=[gffn_b])
        uT, uT_b = A.alloc([KC, 512], BF16, name="uT3")
        mT, mT_b = A.alloc([KC, 512], BF16, name="mT3")
        hres = [A.alloc([D], F32, name="hres%d" % i) for i in range(4)]
        oTs = [A.alloc([8, 512], BF16, name="oT%d" % i) for i in range(2)]
        hid, hid_b = A.alloc([22, 512], BF16, name="hid")
        acc, acc_b = A.alloc([4, 512], F32, name="acc")
        junk, junk_b = A.alloc([D], F32, name="junk3")
        xn, xn_b = A.alloc([D], F32, name="xn3")
        st, st_b = A.alloc([8], F32, name="st3")
        tmpn = (junk, junk_b, xn, xn_b, st, st_b)
        sgs = [A.alloc([512], F32, name="sg%d" % i) for i in range(2)]
        sg_i = 0
        ot_i = 0
        p3groups = groups if not last else groups[:-1]
        for gidx, (c0, tiles) in enumerate(p3groups):
            if gidx > 0:
                P.barrier()
            ntok = sum(nt for _, nt in tiles)
            for ti, (col, nt) in enumerate(tiles):
                ht, ht_b = hres[ti]
                dma(dq(), ht[0:nt, :], h_src(l, col, nt), writes=[ht_b])
                rmsnorm_T(ht, ht_b, nt, gmix, gmix_b, uT, uT_b, col - c0, tmpn)
            for cb in range(4):
                for i in range(3):
                    oT, oT_b = oTs[ot_i % 2]
                    ot_i += 1
                    if cb == 0 or True:
                        dma(dq(), oT[:, :, 0:ntok], OT[i][:, c0:c0 + ntok].rearrange("(k p) t -> p k t", p=128), writes=[oT_b])
                    wg, wg_b = ws.load(wk_gate[l, i, cb], KC, 512)
                    wbr, wbr_b = ws.load(wk_branch[l, i, cb], 8, 512)
                    for j in range(4):
                        bg = next_bank()
                        for kc in range(KC):
                            P.add("pe", lambda e: e.matmul(ps[:, bg, 0:ntok], wg[:, kc, j * 128:(j + 1) * 128], uT[:, kc, 0:ntok], start=(kc == 0), stop=(kc == KC - 1)),
                                  reads=[wg_b, uT_b], writes=[psb[bg]])
                        bb = next_bank()
                        for kc in range(8):
                            P.add("pe", lambda e: e.matmul(ps[:, bb, 0:ntok], wbr[:, kc, j * 128:(j + 1) * 128], oT[:, kc, 0:ntok], start=(kc == 0), stop=(kc == 7)),
                                  reads=[wbr_b, oT_b], writes=[psb[bb]])
                        sg, sg_b = sgs[sg_i % 2]
                        sg_i += 1
                        P.add("act", lambda e: e.activation(out=sg[:, 0:ntok], in_=ps[:, bg, 0:ntok], func=AF.Sigmoid), reads=[psb[bg]], writes=[sg_b])
                        if i == 0:
                            P.add("dve", lambda e: e.tensor_tensor(out=acc[:, j, 0:ntok], in0=ps[:, bb, 0:ntok], in1=sg[:, 0:ntok], op=ALU.mult), reads=[psb[bb], sg_b], writes=[acc_b])
                        else:
                            P.add("dve", lambda e: e.tensor_tensor(out=sg[:, 0:ntok], in0=ps[:, bb, 0:ntok], in1=sg[:, 0:ntok], op=ALU.mult), reads=[psb[bb], sg_b], writes=[sg_b])
                            if i == 1:
                                P.add("pool", lambda e: e.tensor_tensor(out=acc[:, j, 0:ntok], in0=acc[:, j, 0:ntok], in1=sg[:, 0:ntok], op=ALU.add), reads=[acc_b, sg_b], writes=[acc_b])
                            else:
                                P.add("pool", lambda e: e.tensor_tensor(out=mT[:, cb * 4 + j, 0:ntok], in0=acc[:, j, 0:ntok], in1=sg[:, 0:ntok], op=ALU.add), reads=[acc_b, sg_b], writes=[mT_b])
            for nb in range(4):
                wo, wo_b = ws.load(wk_out[l, nb], KC, 512)
                for ti, (col, nt) in enumerate(tiles):
                    ht, ht_b = hres[ti]
                    tc0 = col - c0
                    bk = next_bank()
                    for kc in range(KC):
                        P.add("pe", lambda e: e.matmul(ps[0:nt, bk, :], mT[:, kc, tc0:tc0 + nt], wo[:, kc, :], start=(kc == 0), stop=(kc == KC - 1)),
                              reads=[mT_b, wo_b], writes=[psb[bk]])
                    P.add("dve", lambda e: e.tensor_tensor(out=ht[0:nt, nb * 512:(nb + 1) * 512], in0=ps[0:nt, bk, :], in1=ht[0:nt, nb * 512:(nb + 1) * 512], op=ALU.add),
                          reads=[psb[bk], ht_b], writes=[ht_b])
            for ti, (col, nt) in enumerate(tiles):
                ht, ht_b = hres[ti]
                rmsnorm_T(ht, ht_b, nt, gffn, gffn_b, uT, uT_b, col - c0, tmpn)
            for half in range(2):
                for b11 in range(11):
                    gcol = half * 2816 + b11 * 256
                    wgt, wgt_b = ws.load(wk_fin[l, half * 11 + b11], KC, 256)
                    wup, wup_b = ws.load(wk_fin[l, 22 + half * 11 + b11], KC, 256)
                    for j in range(2):
                        bg = next_bank()
                        for kc in range(KC):
                            P.add("pe", lambda e: e.matmul(ps[:, bg, 0:ntok], wgt[:, kc, j * 128:(j + 1) * 128], uT[:, kc, 0:ntok], start=(kc == 0), stop=(kc == KC - 1)),
                                  reads=[wgt_b, uT_b], writes=[psb[bg]])
                        bu = next_bank()
                        for kc in range(KC):
                            P.add("pe", lambda e: e.matmul(ps[:, bu, 0:ntok], wup[:, kc, j * 128:(j + 1) * 128], uT[:, kc, 0:ntok], start=(kc == 0), stop=(kc == KC - 1)),
                                  reads=[wup_b, uT_b], writes=[psb[bu]])
                        sg, sg_b = sgs[sg_i % 2]
                        sg_i += 1
                        P.add("act", lambda e: e.activation(out=sg[:, 0:ntok], in_=ps[:, bg, 0:ntok], func=AF.Silu), reads=[psb[bg]], writes=[sg_b])
                        P.add("dve", lambda e: e.tensor_tensor(out=hid[:, b11 * 2 + j, 0:ntok], in0=ps[:, bu, 0:ntok], in1=sg[:, 0:ntok], op=ALU.mult), reads=[psb[bu], sg_b], writes=[hid_b])
                for nb in range(4):
                    r0 = half * 2816
                    wa, wa_b = ws.load(wk_fout[l, (half * 2) * 4 + nb], 11, 512)
                    wc, wc_b = ws.load(wk_fout[l, (half * 2 + 1) * 4 + nb], 11, 512)
                    for ti, (col, nt) in enumerate(tiles):
                        ht, ht_b = hres[ti]
                        tc0 = col - c0
                        bk = next_bank()
                        for kc in range(22):
                            wsel, wsel_b = (wa, wa_b) if kc < 11 else (wc, wc_b)
                            P.add("pe", lambda e: e.matmul(ps[0:nt, bk, :], hid[:, kc, tc0:tc0 + nt], wsel[:, kc % 11, :], start=(kc == 0), stop=(kc == 21)),
                                  reads=[hid_b, wsel_b], writes=[psb[bk]])
                        P.add("dve", lambda e: e.tensor_tensor(out=ht[0:nt, nb * 512:(nb + 1) * 512], in0=ps[0:nt, bk, :], in1=ht[0:nt, nb * 512:(nb + 1) * 512], op=ALU.add),
                              reads=[psb[bk], ht_b], writes=[ht_b])
            for ti, (col, nt) in enumerate(tiles):
                ht, ht_b = hres[ti]
                if not last:
                    dma(dq(), h1[col:col + nt, :], ht[0:nt, :], reads=[ht_b])
                else:
                    P.add("act", lambda e: e.activation(out=junk[0:nt, :], in_=ht[0:nt, :], func=AF.Square, accum_out=st[0:nt, 4:5]), reads=[ht_b], writes=[junk_b, st_b])
                    P.add("dve", lambda e: e.tensor_scalar(out=st[0:nt, 5:6], in0=st[0:nt, 4:5], scalar1=1.0 / D, scalar2=EPS, op0=ALU.mult, op1=ALU.add), reads=[st_b], writes=[st_b])
                    P.add("act", lambda e: e.activation(out=st[0:nt, 6:7], in_=st[0:nt, 5:6], func=AF.Sqrt), reads=[st_b], writes=[st_b])
                    P.add("dve", lambda e: e.reciprocal(out=st[0:nt, 7:8], in_=st[0:nt, 6:7]), reads=[st_b], writes=[st_b])
                    P.add("dve", lambda e: e.scalar_tensor_tensor(out=xn[0:nt, :], in0=ht[0:nt, :], scalar=st[0:nt, 7:8], in1=gfin[0:nt, :], op0=ALU.mult, op1=ALU.mult),
                          reads=[ht_b, st_b, gfin_b], writes=[xn_b])
                    yb = Buf("y%d" % col)
                    y_bufs.append(yb)
                    dma(dq(), y[col:col + nt, :], xn[0:nt, :], reads=[xn_b], writes=[yb])
        P.barrier()

    P.add("sp", lambda e: e.nop(), reads=y_bufs)
    P.emit(nc)
    es.close()
    return nc


def make_in_maps(N, depth, seqs, inputs):
    consts = host_constants(N)
    f32 = np.float32
    shared = {
        "meta_tokens": np.ascontiguousarray(inputs["meta_tokens"], f32),
        "nm_cols": np.ascontiguousarray(np.asarray(inputs["norm_mix"], f32)[:depth].reshape(depth, KC, 128).transpose(0, 2, 1)),
        "nf_cols": np.ascontiguousarray(np.asarray(inputs["norm_ffn"], f32)[:depth].reshape(depth, KC, 128).transpose(0, 2, 1)),
        "norm_final": np.ascontiguousarray(inputs["norm_final"], f32),
        "w_in": np.ascontiguousarray(np.asarray(inputs["w_in"], f32)[:depth]),
        "w_gate": np.ascontiguousarray(np.asarray(inputs["w_gate"], f32)[:depth]),
        "w_branch": np.ascontiguousarray(np.asarray(inputs["w_branch"], f32)[:depth]),
        "w_out": np.ascontiguousarray(np.asarray(inputs["w_out"], f32)[:depth]),
        "w_ffn_in": np.ascontiguousarray(np.asarray(inputs["w_ffn_in"], f32)[:depth]),
        "w_ffn_out": np.ascontiguousarray(np.asarray(inputs["w_ffn_out"], f32)[:depth]),
        "na_bias9": na_bias_layout(np.asarray(inputs["na_rel_bias"], f32)[:depth]),
        "diff_lambda": np.ascontiguousarray(np.asarray(inputs["diff_lambda"], f32)[:depth].reshape(depth, 256)),
        "subln_col": np.ascontiguousarray(np.asarray(inputs["diff_subln"], f32)[:depth].reshape(depth, 128, 1)),
        "gqa_sink": np.ascontiguousarray(np.asarray(inputs["gqa_sink"], f32)[:depth]),
    }
    shared.update(consts)
    return [dict(shared, x=np.ascontiguousarray(s, f32)) for s in seqs]


_NC_CACHE = {}


def kernel(x_prompt, x_sample, meta_tokens, norm_mix, w_in, na_rel_bias, diff_lambda, diff_subln,
           gqa_sink, w_gate, w_branch, w_out, norm_ffn, w_ffn_in, w_ffn_out, norm_final):
    x_prompt = np.asarray(x_prompt, np.float32)
    x_sample = np.asarray(x_sample, np.float32)
    N = x_prompt.shape[1]
    depth = 2
    inputs = dict(meta_tokens=meta_tokens, norm_mix=norm_mix, w_in=w_in, na_rel_bias=na_rel_bias,
                  diff_lambda=diff_lambda, diff_subln=diff_subln, gqa_sink=gqa_sink, w_gate=w_gate,
                  w_branch=w_branch, w_out=w_out, norm_ffn=norm_ffn, w_ffn_in=w_ffn_in,
                  w_ffn_out=w_ffn_out, norm_final=norm_final)
    seqs = [x_prompt[i] for i in range(x_prompt.shape[0])] + [x_sample[i] for i in range(x_sample.shape[0])]
    nseq = len(seqs)
    while len(seqs) < 8:
        seqs.append(seqs[len(seqs) % nseq])
    if N not in _NC_CACHE:
        _NC_CACHE[N] = build_program(N, depth)
    nc = _NC_CACHE[N]
    in_maps = make_in_maps(N, depth, seqs, inputs)
    res = run_bass_kernel_spmd(nc, in_maps, core_ids=list(range(8)))
    outs = [np.asarray(res.results[i]["y"], np.float32) for i in range(nseq)]
    nb = x_prompt.shape[0]
    y_prompt = np.stack(outs[:nb], axis=0)
    y_sample = np.stack(outs[nb:], axis=0)
    return (y_prompt, y_sample)
```
